# Optimizing a Trainium2 kernel written in Bass

```python
import math
import jax
import jax.numpy as jnp
from jax import lax
import numpy as np


D_MODEL = 2048
BATCH = 1
SEQ = 8192
DEPTH = 4

N_MIXERS = 4
N_HEADS = 16
HEAD_DIM = D_MODEL // N_HEADS
ROPE_THETA = 500000.0
ROPE_FRACTION = 4
Q_BLOCK = 128
RMS_EPS = 1e-6
FFN_HIDDEN = ((8 * D_MODEL + 3 * 256 - 1) // (3 * 256)) * 256
MLA_Q_LORA = D_MODEL // 4
MLA_KV_LORA = D_MODEL // 4
MLA_NOPE = HEAD_DIM
MLA_ROPE = HEAD_DIM // 2
MLA_V = HEAD_DIM
IDX_HEADS = 16
IDX_DIM = 64
IDX_TOPK = 256
DIFF_HEADS = N_HEADS
DIFF_DIM = HEAD_DIM // 2
POS_OFFSET_MAX = 1024

kernel_name = 'hybrid_mla_dsa_diff_fox_trunk'


def _rms_norm(x, g):
    xf = x.astype(jnp.float32)
    y = xf * lax.rsqrt(jnp.mean(xf * xf, axis=-1, keepdims=True) + RMS_EPS)
    return (y * g.astype(jnp.float32)).astype(x.dtype)


def _rope_cos_sin(positions, rot_dim):
    inv_freq = ROPE_THETA ** (-jnp.arange(0, rot_dim, 2, dtype=jnp.float32) / rot_dim)
    ang = positions.astype(jnp.float32)[..., None] * inv_freq
    return jnp.cos(ang)[:, None], jnp.sin(ang)[:, None]


def _apply_rope(x, cos, sin):
    half = cos.shape[-1]
    x1 = x[..., :half].astype(jnp.float32)
    x2 = x[..., half:2 * half].astype(jnp.float32)
    rotated = jnp.concatenate([x1 * cos - x2 * sin, x2 * cos + x1 * sin], axis=-1).astype(x.dtype)
    return jnp.concatenate([rotated, x[..., 2 * half:]], axis=-1)


def _split_heads(t, n_heads):
    b, s, _ = t.shape
    return t.reshape(b, s, n_heads, -1).transpose(0, 2, 1, 3)


def _merge_heads(t):
    b, h, s, d = t.shape
    return t.transpose(0, 2, 1, 3).reshape(b, s, h * d)


def _causal_mask(start, n_q, n_k):
    q_pos = start + jnp.arange(n_q)
    return q_pos[:, None] >= jnp.arange(n_k)[None, :]


def _sweep_query_blocks(block_fn, q_side):
    s = q_side[0].shape[2]
    nb = s // Q_BLOCK

    def split(a):
        a = a.reshape(a.shape[:2] + (nb, Q_BLOCK) + a.shape[3:])
        return jnp.moveaxis(a, 2, 0)

    starts = jnp.arange(nb, dtype=jnp.int32) * Q_BLOCK
    out = lax.map(lambda xs: block_fn(xs[0], *xs[1]), (starts, tuple(split(a) for a in q_side)))
    out = jnp.moveaxis(out, 0, 2)
    return out.reshape(out.shape[:2] + (nb * Q_BLOCK,) + out.shape[4:])


def _dense_causal_attention(q, k, v, scale):
    s = k.shape[2]

    def block(start, qb):
        logits = jnp.einsum('bhqd,bhkd->bhqk', qb, k).astype(jnp.float32) * scale
        logits = jnp.where(_causal_mask(start, Q_BLOCK, s), logits, -jnp.inf)
        p = jax.nn.softmax(logits, axis=-1)
        return jnp.einsum('bhqk,bhkd->bhqd', p.astype(v.dtype), v)

    return _sweep_query_blocks(block, (q,))


def _mla_mixer(h, rope_mla, w_in, q_a_g, kv_a_g, w_q_b, w_kv_b, q_g, k_g, w_out):
    b, s, _ = h.shape
    c_q, c_kv, k_pe = jnp.split(h @ w_in, [MLA_Q_LORA, MLA_Q_LORA + MLA_KV_LORA], axis=-1)
    q = _split_heads(_rms_norm(c_q, q_a_g) @ w_q_b, N_HEADS)
    kv = _split_heads(_rms_norm(c_kv, kv_a_g) @ w_kv_b, N_HEADS)
    k_nope, v = jnp.split(kv, [MLA_NOPE], axis=-1)
    k_pe = jnp.broadcast_to(k_pe[:, None], (b, N_HEADS, s, MLA_ROPE))
    k = jnp.concatenate([k_pe, k_nope], axis=-1)
    cos, sin = rope_mla
    q = _apply_rope(_rms_norm(q, q_g), cos, sin)
    k = _apply_rope(_rms_norm(k, k_g), cos, sin)
    o = _dense_causal_attention(q, k, v, (MLA_ROPE + MLA_NOPE) ** -0.5)
    return _merge_heads(o) @ w_out


def _dsa_mixer(h, rope_head, rope_idx, w_in, q_g, k_g, idx_k_g, w_out):
    b, s, _ = h.shape
    hd = N_HEADS * HEAD_DIM
    ih = IDX_HEADS * IDX_DIM
    q, k, v, iq, ik, iw = jnp.split(h @ w_in, [hd, 2 * hd, 3 * hd, 3 * hd + ih, 3 * hd + ih + IDX_DIM], axis=-1)
    cos, sin = rope_head
    q = _apply_rope(_rms_norm(_split_heads(q, N_HEADS), q_g), cos, sin)
    k = _apply_rope(_rms_norm(_split_heads(k, N_HEADS), k_g), cos, sin)
    v = _split_heads(v, N_HEADS)
    icos, isin = rope_idx
    iq = _apply_rope(_split_heads(iq, IDX_HEADS), icos, isin)
    ik = _apply_rope(_rms_norm(ik, idx_k_g)[:, None], icos, isin)[:, 0]
    iw = (iw.astype(jnp.float32) * IDX_HEADS ** -0.5).transpose(0, 2, 1)
    n_sel = min(IDX_TOPK, s // 4)
    scale = HEAD_DIM ** -0.5
    iscale = IDX_DIM ** -0.5
    take = jax.vmap(lambda t, idx: t[:, idx])

    def block(start, qb, iqb, iwb):
        q_pos = start + jnp.arange(Q_BLOCK)
        rel = jax.nn.relu(jnp.einsum('bhqd,bkd->bhqk', iqb, ik).astype(jnp.float32) * iscale)
        score = jnp.einsum('bhq,bhqk->bqk', iwb, rel)
        score = jnp.where(_causal_mask(start, Q_BLOCK, s)[None], score, -jnp.inf)
        _, idx = lax.top_k(score, n_sel)
        kg = take(k, idx)
        vg = take(v, idx)
        logits = jnp.einsum('bhqd,bhqnd->bhqn', qb, kg).astype(jnp.float32) * scale
        valid = idx <= q_pos[None, :, None]
        p = jax.nn.softmax(jnp.where(valid[:, None], logits, -jnp.inf), axis=-1)
        return jnp.einsum('bhqn,bhqnd->bhqd', p.astype(vg.dtype), vg)

    o = _sweep_query_blocks(block, (q, iq, iw))
    return _merge_heads(o) @ w_out


def _diff_mixer(h, rope_diff, layer_idx, w_in, q_g, k_g, lq1, lk1, lq2, lk2, subln_g, w_out):
    b, s, _ = h.shape
    w = DIFF_HEADS * 2 * DIFF_DIM
    q, k, v = jnp.split(h @ w_in, [w, 2 * w], axis=-1)
    cos, sin = rope_diff
    q = _apply_rope(_rms_norm(_split_heads(q, 2 * DIFF_HEADS), q_g), cos, sin)
    k = _apply_rope(_rms_norm(_split_heads(k, 2 * DIFF_HEADS), k_g), cos, sin)
    v = _split_heads(v, DIFF_HEADS)
    lam_init = 0.8 - 0.6 * math.exp(-0.3 * layer_idx)
    f32 = jnp.float32
    lam = (jnp.exp(jnp.sum(lq1.astype(f32) * lk1.astype(f32)))
           - jnp.exp(jnp.sum(lq2.astype(f32) * lk2.astype(f32))) + lam_init)
    scale = DIFF_DIM ** -0.5

    def block(start, qb):
        logits = jnp.einsum('bhqd,bhkd->bhqk', qb, k).astype(f32) * scale
        logits = jnp.where(_causal_mask(start, Q_BLOCK, s), logits, -jnp.inf)
        p = jax.nn.softmax(logits, axis=-1).reshape(b, DIFF_HEADS, 2, Q_BLOCK, s)
        a = p[:, :, 0] - lam * p[:, :, 1]
        return jnp.einsum('bhqk,bhkd->bhqd', a.astype(v.dtype), v)

    o = _sweep_query_blocks(block, (q,))
    o = _rms_norm(o, subln_g) * (1.0 - lam_init)
    return _merge_heads(o) @ w_out


def _fox_mixer(h, w_in, b_f, q_g, k_g, w_out):
    b, s, _ = h.shape
    hd = N_HEADS * HEAD_DIM
    q, k, v, f, g = jnp.split(h @ w_in, [hd, 2 * hd, 3 * hd, 3 * hd + N_HEADS], axis=-1)
    q = _rms_norm(_split_heads(q, N_HEADS), q_g)
    k = _rms_norm(_split_heads(k, N_HEADS), k_g)
    v = _split_heads(v, N_HEADS)
    log_f = jax.nn.log_sigmoid(f.astype(jnp.float32) + b_f.astype(jnp.float32))
    cum = lax.cumsum(log_f, axis=1).transpose(0, 2, 1)
    scale = HEAD_DIM ** -0.5

    def block(start, qb, cq):
        logits = (jnp.einsum('bhqd,bhkd->bhqk', qb, k).astype(jnp.float32) * scale
                  + cq[..., None] - cum[:, :, None, :])
        logits = jnp.where(_causal_mask(start, Q_BLOCK, s), logits, -jnp.inf)
        p = jax.nn.softmax(logits, axis=-1)
        return jnp.einsum('bhqk,bhkd->bhqd', p.astype(v.dtype), v)

    o = _sweep_query_blocks(block, (q, cum))
    o = _merge_heads(o) * jax.nn.sigmoid(g)
    return o @ w_out


def _swiglu(h, w_gate_up, w_down):
    gate, up = jnp.split(h @ w_gate_up, 2, axis=-1)
    return (jax.nn.silu(gate) * up) @ w_down


def setup_inputs(seed: int = 0) -> dict:
    key = jax.random.key(seed)
    ks = iter(jax.random.split(key, 48))

    def normal(shape, scale):
        return jax.random.normal(next(ks), shape, jnp.float32) * scale

    def gain(shape):
        return 1.0 + normal(shape, 0.02)

    n_a, n_b, n_c, n_d = [len(range(m, DEPTH, N_MIXERS)) for m in range(N_MIXERS)]
    d = D_MODEL
    hd = N_HEADS * HEAD_DIM
    x = normal((BATCH, SEQ, d), 1.0)
    c = normal((BATCH, d), 1.0)
    positions = (jax.random.randint(next(ks), (BATCH, 1), 0, POS_OFFSET_MAX, dtype=jnp.int32)
                 + jnp.arange(SEQ, dtype=jnp.int32)[None, :])
    mla_in = MLA_Q_LORA + MLA_KV_LORA + MLA_ROPE
    dsa_in = 3 * hd + IDX_HEADS * IDX_DIM + IDX_DIM + IDX_HEADS
    diff_w = DIFF_HEADS * 2 * DIFF_DIM
    fox_in = 3 * hd + N_HEADS + hd
    return {
        'x': x,
        'c': c,
        'positions': positions,
        'ln_mix_g': gain((DEPTH, d)),
        'ln_ffn_g': gain((DEPTH, d)),
        'ada_w': normal((DEPTH, d, 6 * d), 0.5 * d ** -0.5),
        'ada_b': normal((DEPTH, 6 * d), 0.02),
        'ffn_w_gate_up': normal((DEPTH, d, 2 * FFN_HIDDEN), d ** -0.5),
        'ffn_w_down': normal((DEPTH, FFN_HIDDEN, d), FFN_HIDDEN ** -0.5),
        'mla_w_in': normal((n_a, d, mla_in), d ** -0.5),
        'mla_q_a_g': gain((n_a, MLA_Q_LORA)),
        'mla_kv_a_g': gain((n_a, MLA_KV_LORA)),
        'mla_w_q_b': normal((n_a, MLA_Q_LORA, N_HEADS * (MLA_ROPE + MLA_NOPE)), MLA_Q_LORA ** -0.5),
        'mla_w_kv_b': normal((n_a, MLA_KV_LORA, N_HEADS * (MLA_NOPE + MLA_V)), MLA_KV_LORA ** -0.5),
        'mla_q_g': gain((n_a, MLA_ROPE + MLA_NOPE)),
        'mla_k_g': gain((n_a, MLA_ROPE + MLA_NOPE)),
        'mla_w_out': normal((n_a, N_HEADS * MLA_V, d), (N_HEADS * MLA_V) ** -0.5),
        'dsa_w_in': normal((n_b, d, dsa_in), d ** -0.5),
        'dsa_q_g': gain((n_b, HEAD_DIM)),
        'dsa_k_g': gain((n_b, HEAD_DIM)),
        'dsa_idx_k_g': gain((n_b, IDX_DIM)),
        'dsa_w_out': normal((n_b, hd, d), hd ** -0.5),
        'diff_w_in': normal((n_c, d, 3 * diff_w), d ** -0.5),
        'diff_q_g': gain((n_c, DIFF_DIM)),
        'diff_k_g': gain((n_c, DIFF_DIM)),
        'diff_lambda_q1': normal((n_c, DIFF_DIM), 0.1),
        'diff_lambda_k1': normal((n_c, DIFF_DIM), 0.1),
        'diff_lambda_q2': normal((n_c, DIFF_DIM), 0.1),
        'diff_lambda_k2': normal((n_c, DIFF_DIM), 0.1),
        'diff_subln_g': gain((n_c, 2 * DIFF_DIM)),
        'diff_w_out': normal((n_c, diff_w, d), diff_w ** -0.5),
        'fox_w_in': normal((n_d, d, fox_in), d ** -0.5),
        'fox_b_f': jax.random.uniform(next(ks), (n_d, N_HEADS), jnp.float32, 1.0, 6.0),
        'fox_q_g': gain((n_d, HEAD_DIM)),
        'fox_k_g': gain((n_d, HEAD_DIM)),
        'fox_w_out': normal((n_d, hd, d), hd ** -0.5),
    }


def reference(x, c, positions, ln_mix_g, ln_ffn_g, ada_w, ada_b, ffn_w_gate_up, ffn_w_down,
              mla_w_in, mla_q_a_g, mla_kv_a_g, mla_w_q_b, mla_w_kv_b, mla_q_g, mla_k_g, mla_w_out,
              dsa_w_in, dsa_q_g, dsa_k_g, dsa_idx_k_g, dsa_w_out,
              diff_w_in, diff_q_g, diff_k_g, diff_lambda_q1, diff_lambda_k1, diff_lambda_q2,
              diff_lambda_k2, diff_subln_g, diff_w_out,
              fox_w_in, fox_b_f, fox_q_g, fox_k_g, fox_w_out):
    rope_head = _rope_cos_sin(positions, HEAD_DIM // ROPE_FRACTION)
    rope_idx = _rope_cos_sin(positions, IDX_DIM // ROPE_FRACTION)
    rope_diff = _rope_cos_sin(positions, DIFF_DIM // ROPE_FRACTION)
    rope_mla = _rope_cos_sin(positions, MLA_ROPE)
    cond = jax.nn.silu(c)
    for i in range(DEPTH):
        mod = (cond @ ada_w[i] + ada_b[i])[:, None, :]
        sh1, sc1, g1, sh2, sc2, g2 = jnp.split(mod, 6, axis=-1)
        h = _rms_norm(x, ln_mix_g[i]) * (1.0 + sc1) + sh1
        kind, j = i % N_MIXERS, i // N_MIXERS
        if kind == 0:
            y = _mla_mixer(h, rope_mla, mla_w_in[j], mla_q_a_g[j], mla_kv_a_g[j], mla_w_q_b[j],
                           mla_w_kv_b[j], mla_q_g[j], mla_k_g[j], mla_w_out[j])
        elif kind == 1:
            y = _dsa_mixer(h, rope_head, rope_idx, dsa_w_in[j], dsa_q_g[j], dsa_k_g[j],
                           dsa_idx_k_g[j], dsa_w_out[j])
        elif kind == 2:
            y = _diff_mixer(h, rope_diff, i, diff_w_in[j], diff_q_g[j], diff_k_g[j],
                            diff_lambda_q1[j], diff_lambda_k1[j], diff_lambda_q2[j],
                            diff_lambda_k2[j], diff_subln_g[j], diff_w_out[j])
        else:
            y = _fox_mixer(h, fox_w_in[j], fox_b_f[j], fox_q_g[j], fox_k_g[j], fox_w_out[j])
        x = x + g1 * y
        h = _rms_norm(x, ln_ffn_g[i]) * (1.0 + sc2) + sh2
        x = x + g2 * _swiglu(h, ffn_w_gate_up[i], ffn_w_down[i])
    return x
```

```python
import contextlib
import math
import numpy as np
import ml_dtypes
import concourse.bass as bass
import concourse.mybir as mybir
from concourse.bass_utils import run_bass_kernel_spmd

F32 = mybir.dt.float32
F32R = mybir.dt.float32r
BF16 = mybir.dt.bfloat16
I32 = mybir.dt.int32
AF = mybir.ActivationFunctionType
ALU = mybir.AluOpType
AX = mybir.AxisListType

NCORES = 8
SEQ = 8192
D = 2048
TPC = 1024
NT = 8
NH = 16
HD = 128
FFN = 5632
EPS = 1e-6
DTSIZE = {F32: 4, F32R: 4, BF16: 2, I32: 4}

ENGS = ["pe", "act", "dve", "pool", "sp"]
BLOCK_ATTR = {"pe": "tensor", "act": "scalar", "dve": "vector", "pool": "gpsimd", "sp": "sync"}


class Op:
    __slots__ = ("eng", "fn", "deps", "needs_inc", "inc_val", "is_dma", "dsem", "dval", "guard")

    def __init__(self, eng, fn, is_dma=False):
        self.eng = eng
        self.fn = fn
        self.deps = []
        self.needs_inc = False
        self.inc_val = 0
        self.is_dma = is_dma
        self.dsem = None
        self.dval = 0
        self.guard = None


class Sched:
    def __init__(self, nc, n_dma_sems=32):
        self.nc = nc
        self.ops = {e: [] for e in ENGS}
        self.last_w = {}
        self.readers = {}
        self.n_dma_sems = n_dma_sems
        self.dma_rr = 0
        self.dma_last = [None] * n_dma_sems
        self.dma_cnt = [0] * n_dma_sems

    def _add(self, op, reads, writes):
        deps = set()
        for k in reads:
            w = self.last_w.get(k)
            if w is not None:
                deps.add(w)
        for k in writes:
            w = self.last_w.get(k)
            if w is not None:
                deps.add(w)
            for r in self.readers.get(k, ()):
                deps.add(r)
        deps.discard(op)
        for d in deps:
            if (not d.is_dma) and d.eng == "pe" and op.eng == "pe" and not op.is_dma:
                continue
            op.deps.append(d)
            if not d.is_dma:
                d.needs_inc = True
        for k in writes:
            self.last_w[k] = op
            self.readers[k] = []
        for k in reads:
            if k not in writes:
                self.readers.setdefault(k, []).append(op)
        self.ops[op.eng].append(op)
        return op

    def op(self, eng, fn, reads=(), writes=()):
        return self._add(Op(eng, fn), tuple(reads), tuple(writes))

    def dma(self, eng, fn, reads=(), writes=()):
        op = Op(eng, fn, is_dma=True)
        s = self.dma_rr
        self.dma_rr = (self.dma_rr + 1) % self.n_dma_sems
        op.dsem = s
        op.guard = self.dma_last[s]
        self.dma_cnt[s] += 16
        op.dval = self.dma_cnt[s]
        self.dma_last[s] = op
        return self._add(op, tuple(reads), tuple(writes))

    def barrier(self):
        lasts = []
        for e in ENGS:
            for op in reversed(self.ops[e]):
                if not op.is_dma and op.fn is not None:
                    lasts.append(op)
                    break
        dmas = [d for d in self.dma_last if d is not None]
        for e in ENGS:
            b = Op(e, None)
            for d in lasts:
                b.deps.append(d)
                d.needs_inc = True
            for d in dmas:
                b.deps.append(d)
            self.ops[e].append(b)
        self.last_w = {}
        self.readers = {}

    def emit(self, final_wait_eng="sp"):
        nc = self.nc
        for e in ENGS:
            c = 0
            for op in self.ops[e]:
                if op.is_dma or op.fn is None:
                    continue
                if op.needs_inc:
                    c += 1
                    op.inc_val = c
        with contextlib.ExitStack() as st:
            esem = {e: st.enter_context(nc.semaphore("s_" + e)) for e in ENGS}
            dsem = [st.enter_context(nc.semaphore("d_%d" % i)) for i in range(self.n_dma_sems)]
            block = st.enter_context(nc.Block())
            sched = self

            def make(e):
                def body(eng):
                    known_e = {x: 0 for x in ENGS}
                    known_d = [0] * sched.n_dma_sems
                    for op in sched.ops[e]:
                        waits = []
                        for d in op.deps:
                            if d.is_dma:
                                waits.append(("d", d.dsem, d.dval))
                            else:
                                waits.append(("e", d.eng, d.inc_val))
                        if op.is_dma and op.guard is not None:
                            waits.append(("d", op.guard.dsem, op.guard.dval))
                        for kind, which, val in waits:
                            if kind == "d":
                                if known_d[which] < val:
                                    eng.wait_ge(dsem[which], val)
                                    known_d[which] = val
                            else:
                                if known_e[which] < val:
                                    eng.wait_ge(esem[which], val)
                                    known_e[which] = val
                        if op.fn is None:
                            continue
                        inst = op.fn(eng)
                        if op.is_dma:
                            inst.then_inc(dsem[op.dsem], 16)
                        elif op.needs_inc:
                            inst.then_inc(esem[e], 1)
                    if e == final_wait_eng:
                        for s in range(sched.n_dma_sems):
                            if sched.dma_cnt[s] > known_d[s]:
                                eng.wait_ge(dsem[s], sched.dma_cnt[s])
                return body

            for e in ENGS:
                if self.ops[e] or e == final_wait_eng:
                    getattr(block, BLOCK_ATTR[e])(make(e))


def own_tiles(r):
    out = []
    for m in range(4):
        out += [16 * m + r, 16 * m + 15 - r]
    return out


def kt_global(kt):
    m, rem = divmod(kt, 16)
    r, par = divmod(rem, 2)
    g = 16 * m + (r if par == 0 else 15 - r)
    return g, r, 2 * m + par


class Buf:
    def __init__(self, ap, key):
        self.ap = ap
        self.key = key

    def __getitem__(self, idx):
        return self.ap[idx]


ARENA_WORDS = 50 * 1024


class Ctx:
    def __init__(self, nc, st):
        self.nc = nc
        self.S = Sched(nc)
        self.arena = st.enter_context(nc.sbuf_tensor("arena", [128, ARENA_WORDS], F32))
        self.off = 0
        self.base = 0
        self.cnt = 0
        self.psum = [st.enter_context(nc.psum_tensor("psb%d" % i, [128, 512], F32)) for i in range(8)]
        self.dram_cnt = 0

    def alloc(self, shape, dt, name="b"):
        per = 1
        for s in shape[1:]:
            per *= s
        nbytes = per * DTSIZE[dt]
        words = (nbytes + 31) // 32 * 8
        assert self.off + words <= ARENA_WORDS, ("SBUF arena overflow", name, self.off, words)
        ap = self.arena[0:shape[0], self.off:self.off + words]
        if dt != F32:
            ap = ap.bitcast(dt)
        ap = ap[:, 0:per]
        if len(shape) == 3:
            ap = ap.rearrange("p (a b) -> p a b", a=shape[1], b=shape[2])
        elif len(shape) == 4:
            ap = ap.rearrange("p (a b c) -> p a b c", a=shape[1], b=shape[2], c=shape[3])
        self.off += words
        self.cnt += 1
        return Buf(ap, "%s#%d" % (name, self.cnt))

    def mark(self):
        return self.off

    def release(self, mark):
        self.S.barrier()
        self.off = mark

    def ps(self, i):
        return self.psum[i]

    def dram(self, name, shape, dt, kind="Internal"):
        return self.nc.dram_tensor(name, list(shape), dt, kind=kind).ap()


def bc_last(ap2d, n):
    return ap2d.unsqueeze(2).to_broadcast([ap2d.shape[0], ap2d.shape[1], n])


def bc_mid(ap2d, a):
    return ap2d.unsqueeze(1).to_broadcast([ap2d.shape[0], a, ap2d.shape[1]])


MIX_NPROJ = {0: 1088, 1: 7248, 2: 6144, 3: 8208}
NEG = -1.0e30
ROPE_HALF = {0: 32, 1: 16, 2: 8}
IDX_HALF = 8
TWO_PI_HI = 6.28125
TWO_PI_LO = 2.0 * math.pi - 6.28125


class Prog:
    def __init__(self, nc, st):
        self.nc = nc
        self.c = Ctx(nc, st)
        self.S = self.c.S
        self.tin = {}
        self.tout = {}

    def ext_in(self, name, shape, dt):
        ap = self.nc.dram_tensor(name, list(shape), dt, kind="ExternalInput").ap()
        self.tin[name] = (ap, list(shape), dt)
        return ap

    def ext_out(self, name, shape, dt):
        ap = self.nc.dram_tensor(name, list(shape), dt, kind="ExternalOutput").ap()
        self.tout[name] = (ap, list(shape), dt)
        return ap

    def scratch(self, name, shape, dt):
        return self.nc.dram_tensor(name, list(shape), dt, kind="Internal").ap()

    def ACT(self, out, in_, func, reads, writes, **kw):
        return self.S.op("act", lambda e: e.activation(out=out, in_=in_, func=func, **kw), reads, writes)

    def TT(self, eng, out, in0, in1, op, reads, writes):
        return self.S.op(eng, lambda e: e.tensor_tensor(out=out, in0=in0, in1=in1, op=op), reads, writes)

    def TS(self, eng, out, in0, s1, s2, op0, op1, reads, writes):
        if op1 is None:
            return self.S.op(eng, lambda e: e.tensor_scalar(out=out, in0=in0, scalar1=s1, scalar2=None, op0=op0), reads, writes)
        return self.S.op(eng, lambda e: e.tensor_scalar(out=out, in0=in0, scalar1=s1, scalar2=s2, op0=op0, op1=op1), reads, writes)

    def STT(self, out, in0, scalar, in1, op0, op1, reads, writes):
        return self.S.op("dve", lambda e: e.scalar_tensor_tensor(out=out, in0=in0, scalar=scalar, in1=in1, op0=op0, op1=op1), reads, writes)

    def COPY(self, eng, out, in_, reads, writes):
        if eng == "act":
            return self.S.op("act", lambda e: e.activation(out=out, in_=in_, func=AF.Copy), reads, writes)
        return self.S.op(eng, lambda e: e.tensor_copy(out=out, in_=in_), reads, writes)

    def MEMSET(self, eng, out, val, writes):
        return self.S.op(eng, lambda e: e.memset(out, val), (), writes)

    def DMA(self, out, in_, reads, writes, eng="sp", slow=False):
        if slow:
            return self.S.dma(eng, lambda e: e.dma_start(out=out, in_=in_, allow_slow_non_contiguous=True), reads, writes)
        return self.S.dma(eng, lambda e: e.dma_start(out=out, in_=in_), reads, writes)

    def MM(self, out, lhsT, rhs, start, stop, reads, writes):
        return self.S.op("pe", lambda e: e.matmul(out, lhsT=lhsT, rhs=rhs, start=start, stop=stop), reads, writes)

    def TR(self, out, in_, ident, reads, writes):
        return self.S.op("pe", lambda e: e.transpose(out, in_, ident), reads, writes)

    def setup_consts(self):
        c = self.c
        consts_ap = self.ext_in("consts", [128, 512], F32)
        self.cst = c.alloc([128, 512], F32, "cst")
        self.DMA(self.cst.ap, consts_ap, [], [self.cst.key])
        self.ident = self.cst.ap[:, 0:128]
        self.identb = c.alloc([128, 128], BF16, "identb")
        self.COPY("dve", self.identb.ap, self.ident, [self.cst.key], [self.identb.key])
        self.eps = c.alloc([128, 1], F32, "eps")
        self.MEMSET("dve", self.eps.ap, EPS, [self.eps.key])
        self.acc_ctr = 0

    def setup_rope(self, halves):
        c = self.c
        pos_ap = self.ext_in("positions", [TPC], I32)
        posi = c.alloc([128, NT], I32, "posi")
        self.DMA(posi.ap, pos_ap.rearrange("(t p) -> p t", p=128), [], [posi.key], slow=True)
        posf = c.alloc([128, NT], F32, "posf")
        self.COPY("dve", posf.ap, posi.ap, [posi.key], [posf.key])
        self.rope = {}
        for name, (half, off) in halves.items():
            invf = self.cst.ap[:, off:off + half]
            ang = c.alloc([128, NT, half], F32, "ang")
            self.TT("dve", ang.ap, bc_last(posf.ap, half), bc_mid(invf, NT), ALU.mult, [posf.key, self.cst.key], [ang.key])
            outs = []
            for which in ("cos", "sin"):
                a = c.alloc([128, NT, half], F32, "a_" + which)
                if which == "cos":
                    self.TS("dve", a.ap, ang.ap, math.pi / 2.0, None, ALU.add, None, [ang.key], [a.key])
                else:
                    self.COPY("dve", a.ap, ang.ap, [ang.key], [a.key])
                u = c.alloc([128, NT, half], F32, "u")
                ni = c.alloc([128, NT, half], I32, "ni")
                nf = c.alloc([128, NT, half], F32, "nf")
                self.TS("dve", u.ap, a.ap, 1.0 / (2.0 * math.pi), None, ALU.mult, None, [a.key], [u.key])
                self.COPY("dve", ni.ap, u.ap, [u.key], [ni.key])
                self.COPY("dve", nf.ap, ni.ap, [ni.key], [nf.key])
                self.STT(a.ap, nf.ap, -TWO_PI_HI, a.ap, ALU.mult, ALU.add, [nf.key, a.key], [a.key])
                self.STT(a.ap, nf.ap, -TWO_PI_LO, a.ap, ALU.mult, ALU.add, [nf.key, a.key], [a.key])
                self.TS("dve", u.ap, a.ap, math.pi, -2.0 * math.pi, ALU.is_gt, ALU.mult, [a.key], [u.key])
                self.TT("dve", a.ap, a.ap, u.ap, ALU.add, [a.key, u.key], [a.key])
                self.TS("dve", u.ap, a.ap, -math.pi, 2.0 * math.pi, ALU.is_lt, ALU.mult, [a.key], [u.key])
                self.TT("dve", a.ap, a.ap, u.ap, ALU.add, [a.key, u.key], [a.key])
                self.TS("dve", a.ap, a.ap, math.pi, -math.pi, ALU.min, ALU.max, [a.key], [a.key])
                self.ACT(a.ap, a.ap, AF.Sin, [a.key], [a.key])
                outs.append(a)
            self.rope[name] = (outs[0], outs[1])

    def apply_rope(self, t, name, H, a_ap, b_ap, da_ap, db_ap, rkeys, wkeys, tmp):
        cosb, sinb = self.rope[name]
        half = cosb.ap.shape[2]
        cb = bc_mid(cosb.ap[:, t, :], H)
        sb = bc_mid(sinb.ap[:, t, :], H)
        tv = [x.ap[:, 0:H * half].rearrange("p (h f) -> p h f", h=H, f=half) for x in tmp]
        rk = list(rkeys) + [cosb.key, sinb.key]
        self.TT("dve", tv[0], a_ap, cb, ALU.mult, rk, [tmp[0].key])
        self.TT("dve", tv[1], b_ap, sb, ALU.mult, rk, [tmp[1].key])
        self.TT("dve", tv[2], b_ap, cb, ALU.mult, rk, [tmp[2].key])
        self.TT("dve", tv[3], a_ap, sb, ALU.mult, rk, [tmp[3].key])
        self.TT("dve", da_ap, tv[0], tv[1], ALU.subtract, [tmp[0].key, tmp[1].key], wkeys)
        self.TT("dve", db_ap, tv[2], tv[3], ALU.add, [tmp[2].key, tmp[3].key], wkeys)

    def rstd_groups(self, src_ap, src_keys, a, n, scratch, out, total_n=None, extra=None):
        sc3 = scratch.ap[:, 0:a * n].rearrange("p (a n) -> p a n", a=a, n=n)
        self.ACT(sc3, src_ap, AF.Square, src_keys, [scratch.key])
        self.S.op("dve", lambda e: e.tensor_reduce(out=out.ap, in_=sc3, axis=AX.X, op=ALU.add), [scratch.key], [out.key])
        if extra is not None:
            self.TS("dve", out.ap, out.ap, extra[0], None, ALU.add, None, [out.key, extra[1]], [out.key])
        tn = total_n if total_n is not None else n
        self.ACT(out.ap, out.ap, AF.Sqrt, [out.key, self.eps.key], [out.key], scale=1.0 / tn, bias=self.eps.ap)
        self.S.op("dve", lambda e: e.reciprocal(out=out.ap, in_=out.ap), [out.key], [out.key])

    def transpose_f32(self, src_ap, src_keys, nchunk, xT, chunk0, tcol, psbanks):
        c = self.c
        for gi, g0 in enumerate(range(0, nchunk, 8)):
            ng = min(8, nchunk - g0)
            b = psbanks[gi % len(psbanks)]
            psb = c.ps(b)[:].bitcast(BF16)
            for i in range(ng):
                self.TR(psb[:, i * 128:(i + 1) * 128], src_ap[:, (g0 + i) * 128:(g0 + i + 1) * 128], self.identb.ap,
                        list(src_keys) + [self.identb.key], [("ps", b)])
            dst = xT.ap[:, chunk0 + g0:chunk0 + g0 + ng, tcol:tcol + 128]
            srcp = psb[:, 0:ng * 128].rearrange("p (a b) -> p a b", a=ng, b=128)
            self.COPY("act" if gi % 2 == 0 else "dve", dst, srcp, [("ps", b)], [xT.key])

    def linear_tm(self, xT, kc0, KC, ntile, W_ap, N, epilogue, psbanks, wbufs, nw_max=512):
        c = self.c
        Wv = W_ap.rearrange("(kc p) n -> p kc n", p=128)
        pi = 0
        for ci, n0 in enumerate(range(0, N, nw_max)):
            nw = min(nw_max, N - n0)
            wb = wbufs[ci % len(wbufs)]
            self.DMA(wb.ap[:, 0:KC, 0:nw], Wv[:, :, n0:n0 + nw], [], [wb.key], eng="pool")
            for t in range(ntile):
                b = psbanks[pi % len(psbanks)]
                pi += 1
                ps = c.ps(b)
                for kc in range(KC):
                    self.MM(ps[:, 0:nw], xT.ap[:, kc0 + kc, t * 128:(t + 1) * 128], wb.ap[:, kc, 0:nw], kc == 0, kc == KC - 1,
                            [xT.key, wb.key], [("ps", b)])
                epilogue(t, n0, nw, ps[:, 0:nw], b)

    def stage_mod(self, c_ap, adaw_ap, adab_ap, MOD, col0, col1):
        c = self.c
        mk = c.mark()
        cT = c.alloc([128, 16], F32, "cT")
        self.DMA(cT.ap, c_ap.rearrange("o (k p) -> p (o k)", p=128), [], [cT.key], slow=True)
        self.ACT(cT.ap, cT.ap, AF.Silu, [cT.key], [cT.key])
        crep = c.alloc([128, 16, 128], BF16, "crep")
        self.COPY("dve", crep.ap, bc_last(cT.ap, 128), [cT.key], [crep.key])
        wb = [c.alloc([128, 16, 512], BF16, "adaw") for _ in range(2)]
        bb = [c.alloc([128, 512], F32, "adab") for _ in range(2)]
        ob = [c.alloc([128, 512], F32, "adao") for _ in range(2)]
        cnt = [0]
        b2 = adab_ap.rearrange("(o n) -> o n", o=1)

        def epi(t, n0, nw, ps, bank):
            i = cnt[0] % 2
            cnt[0] += 1
            self.DMA(bb[i].ap[:, 0:nw], b2[:, col0 + n0:col0 + n0 + nw].broadcast_to([128, nw]), [], [bb[i].key])
            self.TT("dve", ob[i].ap[:, 0:nw], ps, bb[i].ap[:, 0:nw], ALU.add, [("ps", bank), bb[i].key], [ob[i].key])
            self.DMA(MOD[:, col0 + n0:col0 + n0 + nw], ob[i].ap[:, 0:nw], [ob[i].key], [("MOD", (col0 + n0) // 512)])

        self.linear_tm(crep, 0, 16, 1, adaw_ap[:, col0:col1], col1 - col0, epi, [0, 1], wb)
        c.release(mk)

    def stage_norm(self, XS, lng_ap, MOD, sh_col, sc_col, hT):
        c = self.c
        mk = c.mark()
        G = c.alloc([128, D], F32, "G")
        SH = c.alloc([128, D], F32, "SH")
        gtmp = c.alloc([128, D], F32, "gtmp")
        modkeys = [("MOD", k) for k in range(24)]
        self.DMA(gtmp.ap, lng_ap.rearrange("(o n) -> o n", o=1).broadcast_to([128, D]), [], [gtmp.key])
        self.DMA(G.ap, MOD[:, sc_col:sc_col + D], modkeys, [G.key])
        self.DMA(SH.ap, MOD[:, sh_col:sh_col + D], modkeys, [SH.key])
        self.STT(G.ap, G.ap, 1.0, gtmp.ap, ALU.add, ALU.mult, [G.key, gtmp.key], [G.key])
        xb = [c.alloc([128, D], F32, "xb") for _ in range(2)]
        hb = [c.alloc([128, D], F32, "hb") for _ in range(2)]
        hbb = [c.alloc([128, D], BF16, "hbb") for _ in range(2)]
        ss = [c.alloc([128, 1], F32, "ss") for _ in range(2)]
        scr = gtmp
        for t in range(NT):
            i = t % 2
            self.DMA(xb[i].ap, XS[t * 128:(t + 1) * 128, :], [("XS", t)], [xb[i].key])
            self.MEMSET("dve", ss[i].ap, 0.0, [ss[i].key])
            self.ACT(scr.ap, xb[i].ap, AF.Square, [xb[i].key, ss[i].key, G.key], [scr.key, ss[i].key], accum_out=ss[i].ap)
            self.ACT(ss[i].ap, ss[i].ap, AF.Sqrt, [ss[i].key, self.eps.key], [ss[i].key], scale=1.0 / D, bias=self.eps.ap)
            self.S.op("dve", lambda e, o=ss[i].ap: e.reciprocal(out=o, in_=o), [ss[i].key], [ss[i].key])
            self.STT(hb[i].ap, xb[i].ap, ss[i].ap, G.ap, ALU.mult, ALU.mult, [xb[i].key, ss[i].key, G.key], [hb[i].key])
            self.TT("pool", hbb[i].ap, hb[i].ap, SH.ap, ALU.add, [hb[i].key, SH.key], [hbb[i].key])
            self.transpose_f32(hbb[i].ap, [hbb[i].key], 16, hT, 0, t * 128, [0, 1])
        return mk

    def stage_proj(self, hT, KC, W_ap, N, OUT, outname, psbanks=(2, 3, 4, 5)):
        c = self.c
        mk = c.mark()
        wb = [c.alloc([128, KC, 512], BF16, "wproj") for _ in range(2)]
        ob = [c.alloc([128, 512], F32, "oproj") for _ in range(3)]
        cnt = [0]

        def epi(t, n0, nw, ps, bank):
            i = cnt[0] % 3
            cnt[0] += 1
            self.COPY("act" if cnt[0] % 2 else "dve", ob[i].ap[:, 0:nw], ps, [("ps", bank)], [ob[i].key])
            self.DMA(OUT[t * 128:(t + 1) * 128, n0:n0 + nw], ob[i].ap[:, 0:nw], [ob[i].key], [(outname, t)])

        self.linear_tm(hT, 0, KC, NT, W_ap, N, epi, list(psbanks), wb)
        c.release(mk)

    def heads_to_T(self, srcb, t, nblk, DST, dstname, psbank, stage):
        c = self.c
        psb = c.ps(psbank)[:].bitcast(BF16)
        for g0 in range(0, nblk, 8):
            ng = min(8, nblk - g0)
            for i in range(ng):
                self.TR(psb[:, i * 128:(i + 1) * 128], srcb.ap[:, (g0 + i) * 128:(g0 + i + 1) * 128], self.identb.ap,
                        [srcb.key, self.identb.key], [("ps", psbank)])
            self.COPY("act", stage.ap[:, 0:ng, :], psb[:, 0:ng * 128].rearrange("p (a b) -> p a b", a=ng, b=128), [("ps", psbank)], [stage.key])
            self.DMA(DST[g0:g0 + ng, :, t * 128:(t + 1) * 128].rearrange("u d t -> d u t"), stage.ap[:, 0:ng, :], [stage.key], [(dstname, t)])

    def head_norm(self, t, src3, src_keys, H, n, gain, outb3, out_key, rope_name, bufs, total_n=None, extra=None, rstd_out=None):
        scr, nrm, rs, tmp = bufs["scr"], bufs["nrm"], bufs["rs"], bufs["tmp"]
        rsv = Buf(rs.ap[:, 0:H], rs.key)
        self.rstd_groups(src3, src_keys, H, n, scr, rsv, total_n=total_n, extra=extra)
        n3 = nrm.ap[:, 0:H * n].rearrange("p (h n) -> p h n", h=H, n=n)
        self.TT("dve", n3, src3, bc_last(rsv.ap, n), ALU.mult, list(src_keys) + [rs.key], [nrm.key])
        if gain is not None:
            self.TT("pool", n3, n3, bc_mid(gain.ap, H), ALU.mult, [nrm.key, gain.key], [nrm.key])
        if rope_name is not None:
            half = self.rope[rope_name][0].ap.shape[2]
            self.apply_rope(t, rope_name, H, n3[:, :, 0:half], n3[:, :, half:2 * half], outb3[:, :, 0:half], outb3[:, :, half:2 * half],
                            [nrm.key], [out_key], tmp)
            self.COPY("act", outb3[:, :, 2 * half:n], n3[:, :, 2 * half:n], [nrm.key], [out_key])
        else:
            self.COPY("act", outb3, n3, [nrm.key], [out_key])

    def load_gain(self, g_ap, n, name):
        b = self.c.alloc([128, n], F32, name)
        self.DMA(b.ap, g_ap.rearrange("(o n) -> o n", o=1).broadcast_to([128, n]), [], [b.key])
        return b

    def s3_bufs(self, maxw):
        c = self.c
        return {
            "scr": c.alloc([128, maxw], F32, "scr"),
            "nrm": c.alloc([128, maxw], F32, "nrm"),
            "rs": c.alloc([128, 32], F32, "rs"),
            "tmp": [c.alloc([128, 1024], F32, "rtmp") for _ in range(4)],
        }

    def s3_dsa(self, PROJ, w, out):
        c = self.c
        mk = c.mark()
        qg = self.load_gain(w["dsa_q_g"], 128, "qg")
        kg = self.load_gain(w["dsa_k_g"], 128, "kg")
        ikg = self.load_gain(w["dsa_idx_k_g"], 64, "ikg")
        bufs = self.s3_bufs(2048)
        pt = [c.alloc([128, 7248], F32, "pt") for _ in range(2)]
        qb = [c.alloc([128, 2048], BF16, "qb") for _ in range(2)]
        vb = c.alloc([128, 2048], BF16, "vb")
        iqb = c.alloc([128, 1024], BF16, "iqb")
        ikb = c.alloc([128, 128], BF16, "ikb")
        iwb = c.alloc([128, 16], F32, "iwb")
        stg = [c.alloc([128, 8, 128], BF16, "stg") for _ in range(2)]
        for t in range(NT):
            p = pt[t % 2]
            self.DMA(p.ap, PROJ[t * 128:(t + 1) * 128, :], [("PROJ", t)], [p.key])
            for which, (col, g, DST, nm) in enumerate(((0, qg, out["QT"], "QT"), (2048, kg, out["KT"], "KT"))):
                o = qb[which]
                self.head_norm(t, p.ap[:, col:col + 2048].rearrange("p (h n) -> p h n", h=16, n=128), [p.key], 16, 128, g,
                               o.ap.rearrange("p (h n) -> p h n", h=16, n=128), o.key, "head", bufs)
                self.heads_to_T(o, t, 16, DST, nm, 6 + which, stg[which])
            self.COPY("act", vb.ap, p.ap[:, 4096:6144], [p.key], [vb.key])
            self.DMA(out["V"][t * 128:(t + 1) * 128, :], vb.ap, [vb.key], [("V", t)])
            iq3 = p.ap[:, 6144:7168].rearrange("p (h n) -> p h n", h=16, n=64)
            iqb3 = iqb.ap.rearrange("p (h n) -> p h n", h=16, n=64)
            self.apply_rope(t, "idx", 16, iq3[:, :, 0:8], iq3[:, :, 8:16], iqb3[:, :, 0:8], iqb3[:, :, 8:16], [p.key], [iqb.key], bufs["tmp"])
            self.COPY("act", iqb3[:, :, 16:64], iq3[:, :, 16:64], [p.key], [iqb.key])
            self.heads_to_T(iqb, t, 8, out["IQT"], "IQT", 6, stg[0])
            ik3 = p.ap[:, 7168:7232].rearrange("p (h n) -> p h n", h=1, n=64)
            ikb3 = ikb.ap[:, 0:64].rearrange("p (h n) -> p h n", h=1, n=64)
            self.head_norm(t, ik3, [p.key], 1, 64, ikg, ikb3, ikb.key, "idx", bufs)
            self.COPY("dve", ikb.ap[:, 64:128], ikb.ap[:, 0:64], [ikb.key], [ikb.key])
            self.heads_to_T(ikb, t, 1, out["IKT"], "IKT", 7, stg[1])
            self.TS("dve", iwb.ap, p.ap[:, 7232:7248], 0.25, None, ALU.mult, None, [p.key], [iwb.key])
            self.DMA(out["IW"][t * 128:(t + 1) * 128, :], iwb.ap, [iwb.key], [("IW", t)])
        c.release(mk)

    def s3_diff(self, PROJ, w, out):
        c = self.c
        mk = c.mark()
        qg = self.load_gain(w["diff_q_g"], 64, "qg")
        kg = self.load_gain(w["diff_k_g"], 64, "kg")
        bufs = self.s3_bufs(2048)
        pt = [c.alloc([128, 6144], F32, "pt") for _ in range(2)]
        qb = [c.alloc([128, 2048], BF16, "qb") for _ in range(2)]
        vb = c.alloc([128, 2048], BF16, "vb")
        stg = [c.alloc([128, 8, 128], BF16, "stg") for _ in range(2)]
        for t in range(NT):
            p = pt[t % 2]
            self.DMA(p.ap, PROJ[t * 128:(t + 1) * 128, :], [("PROJ", t)], [p.key])
            for which, (col, g, DST, nm) in enumerate(((0, qg, out["QT"], "QT"), (2048, kg, out["KT"], "KT"))):
                o = qb[which]
                self.head_norm(t, p.ap[:, col:col + 2048].rearrange("p (h n) -> p h n", h=32, n=64), [p.key], 32, 64, g,
                               o.ap.rearrange("p (h n) -> p h n", h=32, n=64), o.key, "diff", bufs)
                self.heads_to_T(o, t, 16, DST, nm, 6 + which, stg[which])
            self.COPY("act", vb.ap, p.ap[:, 4096:6144], [p.key], [vb.key])
            self.DMA(out["V"][t * 128:(t + 1) * 128, :], vb.ap, [vb.key], [("V", t)])
        c.release(mk)

    def s3_fox(self, PROJ, w, out):
        c = self.c
        mk = c.mark()
        qg = self.load_gain(w["fox_q_g"], 128, "qg")
        kg = self.load_gain(w["fox_k_g"], 128, "kg")
        bf = self.load_gain(w["fox_b_f"], 16, "bf")
        bufs = self.s3_bufs(2048)
        pt = [c.alloc([128, 8208], F32, "pt") for _ in range(2)]
        qb = [c.alloc([128, 2048], BF16, "qb") for _ in range(2)]
        vb = c.alloc([128, 2048], BF16, "vb")
        gt = c.alloc([128, 2048], F32, "gt")
        lf = c.alloc([128, 16], F32, "lf")
        lft = c.alloc([16, 128], F32, "lft")
        one = c.alloc([128, 1], F32, "one")
        self.MEMSET("dve", one.ap, 1.0, [one.key])
        stg = [c.alloc([128, 8, 128], BF16, "stg") for _ in range(2)]
        for t in range(NT):
            p = pt[t % 2]
            self.DMA(p.ap, PROJ[t * 128:(t + 1) * 128, :], [("PROJ", t)], [p.key])
            for which, (col, g, DST, nm) in enumerate(((0, qg, out["QT"], "QT"), (2048, kg, out["KT"], "KT"))):
                o = qb[which]
                self.head_norm(t, p.ap[:, col:col + 2048].rearrange("p (h n) -> p h n", h=16, n=128), [p.key], 16, 128, g,
                               o.ap.rearrange("p (h n) -> p h n", h=16, n=128), o.key, None, bufs)
                self.heads_to_T(o, t, 16, DST, nm, 6 + which, stg[which])
            self.COPY("act", vb.ap, p.ap[:, 4096:6144], [p.key], [vb.key])
            self.DMA(out["V"][t * 128:(t + 1) * 128, :], vb.ap, [vb.key], [("V", t)])
            self.TT("dve", lf.ap, p.ap[:, 6144:6160], bf.ap, ALU.add, [p.key, bf.key], [lf.key])
            self.ACT(lf.ap, lf.ap, AF.Exp, [lf.key], [lf.key], scale=-1.0)
            self.ACT(lf.ap, lf.ap, AF.Ln, [lf.key, one.key], [lf.key], bias=one.ap)
            self.TS("dve", lf.ap, lf.ap, -1.0, None, ALU.mult, None, [lf.key], [lf.key])
            ps = c.ps(5)
            self.TR(ps[0:16, 0:128], lf.ap, self.ident, [lf.key, self.cst.key], [("ps", 5)])
            self.COPY("dve", lft.ap, ps[0:16, 0:128], [("ps", 5)], [lft.key])
            self.DMA(out["LOGF"][:, t * 128:(t + 1) * 128], lft.ap, [lft.key], [("LOGF", t)])
            self.ACT(gt.ap, p.ap[:, 6160:8208], AF.Sigmoid, [p.key], [gt.key])
            self.DMA(out["GATE"][t * 128:(t + 1) * 128, :], gt.ap, [gt.key], [("GATE", t)])
        c.release(mk)

    def s3_mla(self, PROJ, w, out, Q2, KV2):
        c = self.c
        mk = c.mark()
        bufs = self.s3_bufs(3072)
        gab = c.alloc([128, 1024], F32, "gab")
        self.DMA(gab.ap[:, 0:512], w["mla_q_a_g"].rearrange("(o n) -> o n", o=1).broadcast_to([128, 512]), [], [gab.key])
        self.DMA(gab.ap[:, 512:1024], w["mla_kv_a_g"].rearrange("(o n) -> o n", o=1).broadcast_to([128, 512]), [], [(gab.key, 1)])
        cT = c.alloc([128, 8, TPC], BF16, "cT")
        mk2 = c.mark()
        pt = [c.alloc([128, 1088], F32, "pt") for _ in range(2)]
        cn = [c.alloc([128, 1024], F32, "cn") for _ in range(2)]
        cnb = [c.alloc([128, 1024], BF16, "cnb") for _ in range(2)]
        for t in range(NT):
            p = pt[t % 2]
            o = cn[t % 2]
            self.DMA(p.ap, PROJ[t * 128:(t + 1) * 128, :], [("PROJ", t)], [p.key])
            src3 = p.ap[:, 0:1024].rearrange("p (h n) -> p h n", h=2, n=512)
            rsv = Buf(bufs["rs"].ap[:, 0:2], bufs["rs"].key)
            self.rstd_groups(src3, [p.key], 2, 512, bufs["scr"], rsv)
            o3 = o.ap.rearrange("p (h n) -> p h n", h=2, n=512)
            self.TT("dve", o3, src3, bc_last(rsv.ap, 512), ALU.mult, [p.key, bufs["rs"].key], [o.key])
            ob_ = cnb[t % 2]
            self.TT("pool", ob_.ap, o.ap, gab.ap, ALU.mult, [o.key, gab.key, (gab.key, 1)], [ob_.key])
            self.transpose_f32(ob_.ap, [ob_.key], 8, cT, 0, t * 128, [0, 1])
        c.release(mk2)
        wb = [c.alloc([128, 4, 512], BF16, "w2") for _ in range(2)]
        ob = [c.alloc([128, 512], F32, "o2") for _ in range(3)]
        cnt = [0]

        def mk_epi(OUT, nm):
            def epi(t, n0, nw, ps, bank):
                i = cnt[0] % 3
                cnt[0] += 1
                self.COPY("act" if cnt[0] % 2 else "dve", ob[i].ap[:, 0:nw], ps, [("ps", bank)], [ob[i].key])
                self.DMA(OUT[t * 128:(t + 1) * 128, n0:n0 + nw], ob[i].ap[:, 0:nw], [ob[i].key], [(nm, t)])
            return epi

        self.linear_tm(cT, 0, 4, NT, w["mla_w_q_b"], 3072, mk_epi(Q2, "Q2"), [2, 3, 4, 5], wb)
        self.linear_tm(cT, 4, 4, NT, w["mla_w_kv_b"], 4096, mk_epi(KV2, "KV2"), [2, 3, 4, 5], wb)
        c.release(mk2)
        qg = self.load_gain(w["mla_q_g"], 192, "qg")
        kg = self.load_gain(w["mla_k_g"], 192, "kg")
        q2 = [c.alloc([128, 3072], F32, "q2") for _ in range(2)]
        kv = [c.alloc([128, 4096], F32, "kv") for _ in range(2)]
        kpe = [c.alloc([128, 64], F32, "kpe") for _ in range(2)]
        sspe = c.alloc([128, 1], F32, "sspe")
        sq64 = c.alloc([128, 64], F32, "sq64")
        qrb = c.alloc([128, 1024], BF16, "qrb")
        qnb = c.alloc([128, 2048], BF16, "qnb")
        knb = c.alloc([128, 2048], BF16, "knb")
        kpb = c.alloc([128, 1024], BF16, "kpb")
        kpf = c.alloc([128, 1024], F32, "kpf")
        vb = c.alloc([128, 2048], BF16, "vb")
        stg = [c.alloc([128, 8, 128], BF16, "stg") for _ in range(2)]
        scr, nrm, rs, tmp = bufs["scr"], bufs["nrm"], bufs["rs"], bufs["tmp"]
        for t in range(NT):
            i = t % 2
            self.DMA(q2[i].ap, Q2[t * 128:(t + 1) * 128, :], [("Q2", t)], [q2[i].key])
            self.DMA(kv[i].ap, KV2[t * 128:(t + 1) * 128, :], [("KV2", t)], [kv[i].key])
            self.DMA(kpe[i].ap, PROJ[t * 128:(t + 1) * 128, 1024:1088], [("PROJ", t)], [kpe[i].key])
            q3 = q2[i].ap.rearrange("p (h n) -> p h n", h=16, n=192)
            rsv = Buf(rs.ap[:, 0:16], rs.key)
            self.rstd_groups(q3, [q2[i].key], 16, 192, scr, rsv)
            n3 = nrm.ap[:, 0:3072].rearrange("p (h n) -> p h n", h=16, n=192)
            self.TT("dve", n3, q3, bc_last(rsv.ap, 192), ALU.mult, [q2[i].key, rs.key], [nrm.key])
            self.TT("pool", n3, n3, bc_mid(qg.ap, 16), ALU.mult, [nrm.key, qg.key], [nrm.key])
            qr3 = qrb.ap.rearrange("p (h n) -> p h n", h=16, n=64)
            self.apply_rope(t, "mla", 16, n3[:, :, 0:32], n3[:, :, 32:64], qr3[:, :, 0:32], qr3[:, :, 32:64], [nrm.key], [qrb.key], tmp)
            self.COPY("act", qnb.ap.rearrange("p (h n) -> p h n", h=16, n=128), n3[:, :, 64:192], [nrm.key], [qnb.key])
            self.heads_to_T(qrb, t, 8, out["QT2"], "QT2", 6, stg[0])
            self.heads_to_T(qnb, t, 16, out["QT"], "QT", 6, stg[0])
            self.MEMSET("dve", sspe.ap, 0.0, [sspe.key])
            self.ACT(sq64.ap, kpe[i].ap, AF.Square, [kpe[i].key, sspe.key], [sq64.key, sspe.key], accum_out=sspe.ap)
            kv3 = kv[i].ap.rearrange("p (h n) -> p h n", h=16, n=256)
            rsv2 = Buf(rs.ap[:, 16:32], (rs.key, 1))
            self.rstd_groups(kv3[:, :, 0:128], [kv[i].key], 16, 128, scr, rsv2, total_n=192, extra=(sspe.ap, sspe.key))
            k3 = nrm.ap[:, 0:2048].rearrange("p (h n) -> p h n", h=16, n=128)
            self.TT("dve", k3, kv3[:, :, 0:128], bc_last(rsv2.ap, 128), ALU.mult, [kv[i].key, (rs.key, 1)], [nrm.key])
            self.TT("pool", knb.ap.rearrange("p (h n) -> p h n", h=16, n=128), k3, bc_mid(kg.ap[:, 64:192], 16), ALU.mult,
                    [nrm.key, kg.key], [knb.key])
            self.heads_to_T(knb, t, 16, out["KT"], "KT", 7, stg[1])
            kp3 = kpf.ap.rearrange("p (h n) -> p h n", h=16, n=64)
            self.TT("dve", kp3, bc_mid(kpe[i].ap, 16), bc_last(rsv2.ap, 64), ALU.mult, [kpe[i].key, (rs.key, 1)], [kpf.key])
            self.TT("pool", kp3, kp3, bc_mid(kg.ap[:, 0:64], 16), ALU.mult, [kpf.key, kg.key], [kpf.key])
            kpb3 = kpb.ap.rearrange("p (h n) -> p h n", h=16, n=64)
            self.apply_rope(t, "mla", 16, kp3[:, :, 0:32], kp3[:, :, 32:64], kpb3[:, :, 0:32], kpb3[:, :, 32:64], [kpf.key], [kpb.key], tmp)
            self.heads_to_T(kpb, t, 8, out["KT2"], "KT2", 7, stg[1])
            self.COPY("act", vb.ap.rearrange("p (h n) -> p h n", h=16, n=128), kv3[:, :, 128:256], [kv[i].key], [vb.key])
            self.DMA(out["V"][t * 128:(t + 1) * 128, :], vb.ap, [vb.key], [("V", t)])
        c.release(mk)

    def s4_dsa_indexer(self, tin, selT, sel_off):
        c = self.c
        mk = c.mark()
        IKs = c.alloc([128, SEQ], BF16, "IKs")
        iksrc = tin["IKT_all"][:, 0].rearrange("r d (m c) -> d m r c", m=4, c=256)
        for m in range(4):
            self.DMA(IKs.ap[:, m * 2048:(m + 1) * 2048].rearrange("d (r c) -> d r c", r=8, c=256), iksrc[:, m], [], [IKs.key])
        IQs = c.alloc([128, 8, TPC], BF16, "IQs")
        self.DMA(IQs.ap, tin["IQT"].rearrange("u d t -> d u t"), [], [IQs.key])
        IWs = c.alloc([128, NT, 16], F32, "IWs")
        self.DMA(IWs.ap, tin["IW"].rearrange("(t p) h -> p t h", p=128), [], [IWs.key])
        PEN = c.alloc([128, 2, 2048], BF16, "PEN")
        self.DMA(PEN.ap, tin["pen"].rearrange("a q k -> q a k"), [], [PEN.key])
        score = c.alloc([128, SEQ], F32, "score")
        selb = c.alloc([128, SEQ], BF16, "selb")
        rl = [c.alloc([128, 512], F32, "rl") for _ in range(3)]
        m8 = c.alloc([128, 8], F32, "m8")
        cnt = 0
        for j in range(NT):
            m, par = divmod(j, 2)
            nk = 2048 * (m + 1)
            for ch in range(nk // 512):
                for h in range(16):
                    b = cnt % 4
                    r = rl[cnt % 3]
                    cnt += 1
                    ps = c.ps(b)
                    base = 64 * (h % 2)
                    self.MM(ps[:, 0:512], IQs.ap[base:base + 64, h // 2, j * 128:(j + 1) * 128], IKs.ap[base:base + 64, ch * 512:(ch + 1) * 512],
                            True, True, [IQs.key, IKs.key], [("ps", b)])
                    self.ACT(r.ap, ps[:, 0:512], AF.Relu, [("ps", b)], [r.key], scale=0.125)
                    sc = score.ap[:, ch * 512:(ch + 1) * 512]
                    if h == 0:
                        self.TS("dve", sc, r.ap, IWs.ap[:, j, 0:1], None, ALU.mult, None, [r.key, IWs.key], [score.key])
                    else:
                        self.STT(sc, r.ap, IWs.ap[:, j, h:h + 1], sc, ALU.mult, ALU.add, [r.key, IWs.key, score.key], [score.key])
            dg = score.ap[:, 2048 * m:2048 * (m + 1)]
            self.TT("pool", dg, dg, PEN.ap[:, par, :], ALU.add, [score.key, PEN.key], [score.key])
            sv = score.ap[:, 0:nk]
            for rd in range(32):
                self.S.op("dve", lambda e, sv=sv: e.max(out=m8.ap, in_=sv), [score.key], [m8.key])
                self.S.op("dve", lambda e, sv=sv: e.match_replace(out=sv, in_to_replace=m8.ap, in_values=sv, imm_value=-3.0e38),
                          [score.key, m8.key], [score.key])
            self.TS("dve", selb.ap[:, 0:nk], sv, -2.0e38, None, ALU.is_le, None, [score.key], [selb.key])
            nkt = 16 * (m + 1)
            for g0 in range(0, nkt, 8):
                b = 4 + (g0 // 8) % 2
                psb = c.ps(b)[:].bitcast(BF16)
                for i in range(8):
                    self.TR(psb[:, i * 128:(i + 1) * 128], selb.ap[:, (g0 + i) * 128:(g0 + i + 1) * 128], self.identb.ap,
                            [selb.key, self.identb.key], [("ps", b)])
                self.COPY("act", selT.ap[:, sel_off[j] + g0:sel_off[j] + g0 + 8, :], psb.rearrange("p (a b) -> p a b", a=8, b=128),
                          [("ps", b)], [selT.key])
        c.release(mk)

    def s4_attention(self, kind, tin, O, w, layer_idx):
        c = self.c
        mk = c.mark()
        scale = {0: 192 ** -0.5, 1: 128 ** -0.5, 2: 64 ** -0.5, 3: 128 ** -0.5}[kind]
        MASK = c.alloc([128, 2, 16, 128], BF16, "MASK")
        self.DMA(MASK.ap, tin["maskT"].rearrange("a t k q -> k a t q"), [], [MASK.key])
        selT = None
        sel_off = []
        if kind == 1:
            off = 0
            for j in range(NT):
                sel_off.append(off)
                off += 16 * (j // 2 + 1)
            selT = c.alloc([128, off, 128], BF16, "selT")
            self.s4_dsa_indexer(tin, selT, sel_off)
        neglam = None
        if kind == 2:
            lam_init = 0.8 - 0.6 * math.exp(-0.3 * layer_idx)
            lv = [self.load_gain(w[n], 64, n) for n in ("diff_lambda_q1", "diff_lambda_k1", "diff_lambda_q2", "diff_lambda_k2")]
            e = [c.alloc([128, 1], F32, "lam_e") for _ in range(2)]
            pr = c.alloc([128, 64], F32, "lam_pr")
            for i in range(2):
                self.TT("dve", pr.ap, lv[2 * i].ap, lv[2 * i + 1].ap, ALU.mult, [lv[2 * i].key, lv[2 * i + 1].key], [pr.key])
                self.S.op("dve", lambda e_, o=e[i].ap: e_.tensor_reduce(out=o, in_=pr.ap, axis=AX.X, op=ALU.add), [pr.key], [e[i].key])
                self.ACT(e[i].ap, e[i].ap, AF.Exp, [e[i].key], [e[i].key])
            neglam = c.alloc([128, 1], F32, "neglam")
            self.TT("dve", neglam.ap, e[1].ap, e[0].ap, ALU.subtract, [e[0].key, e[1].key], [neglam.key])
            self.TS("dve", neglam.ap, neglam.ap, -lam_init, None, ALU.add, None, [neglam.key], [neglam.key])
        NCK = cref = biasb = None
        if kind == 3:
            NCK, cref = self.s4_fox_prelude(tin)
            biasb = [c.alloc([128, 64], F32, "biasb") for _ in range(2)]
        Ks = [c.alloc([128, SEQ], BF16, "Ks") for _ in range(2)]
        Vs = [c.alloc([128, 64, 129], BF16, "Vs") for _ in range(2)]
        Qs = [c.alloc([128, TPC], BF16, "Qs") for _ in range(2)]
        K2s = Q2s = None
        if kind == 0:
            K2s = [c.alloc([128, SEQ], BF16, "K2s") for _ in range(2)]
            Q2s = [c.alloc([128, TPC], BF16, "Q2s") for _ in range(2)]
        PT = [c.alloc([128, 512], BF16, "PT") for _ in range(3)]
        osb = [c.alloc([128, 128], F32, "osb") for _ in range(3)]
        rsm = [c.alloc([128, 1], F32, "rsm") for _ in range(3)]
        for i in range(2):
            self.MEMSET("pool", Vs[i].ap[:, :, 128:129], 1.0, [Vs[i].key])

        def loads(h):
            i = h % 2
            ksrc = tin["KT_all"][:, h].rearrange("r d (m c) -> d m r c", m=4, c=256)
            for m in range(4):
                self.DMA(Ks[i].ap[:, m * 2048:(m + 1) * 2048].rearrange("d (r c) -> d r c", r=8, c=256), ksrc[:, m], [], [Ks[i].key])
            self.DMA(Qs[i].ap, tin["QT"][h], [], [Qs[i].key])
            vsrc = tin["V_all"][:, :, h * 128:(h + 1) * 128].rearrange("r (m a p) c -> p m a r c", m=4, a=2, p=128)
            for m in range(4):
                for a in range(2):
                    self.DMA(Vs[i].ap[:, 16 * m + a:16 * m + 16:2, 0:128], vsrc[:, m, a], [], [Vs[i].key])
            if kind == 0:
                base = 64 * (h % 2)
                k2src = tin["KT2_all"][:, h // 2, base:base + 64, :].rearrange("r d (m c) -> d m r c", m=4, c=256)
                for m in range(4):
                    self.DMA(K2s[i].ap[base:base + 64, m * 2048:(m + 1) * 2048].rearrange("d (r c) -> d r c", r=8, c=256), k2src[:, m],
                             [], [K2s[i].key])
                self.DMA(Q2s[i].ap[base:base + 64, :], tin["QT2"][h // 2, base:base + 64, :], [], [Q2s[i].key])

        gcnt = 0
        ocnt = 0
        loads(0)
        for h in range(16):
            if h + 1 < 16:
                loads(h + 1)
            i = h % 2
            npass = 2 if kind == 2 else 1
            for j in range(NT):
                m, par = divmod(j, 2)
                nkt = 16 * (m + 1)
                if kind == 3:
                    bb = biasb[(h * NT + j) % 2]
                    self.TS("dve", bb.ap[:, 0:nkt], NCK.ap[:, 0:nkt, h], cref.ap[:, j, h:h + 1], 60.0, ALU.add, ALU.min,
                            [NCK.key, cref.key], [bb.key])
                pobanks = []
                for pz in range(npass):
                    pob = 3 + (ocnt % 4)
                    ocnt += 1
                    pobanks.append(pob)
                    po = c.ps(pob)
                    base = 64 * pz
                    for grp in range(nkt // 4):
                        sb = gcnt % 3
                        pt = PT[gcnt % 3]
                        gcnt += 1
                        ps = c.ps(sb)
                        for cc in range(4):
                            kt = 4 * grp + cc
                            if kind == 2:
                                self.MM(ps[:, cc * 128:(cc + 1) * 128], Ks[i].ap[base:base + 64, kt * 128:(kt + 1) * 128],
                                        Qs[i].ap[base:base + 64, j * 128:(j + 1) * 128], True, True, [Ks[i].key, Qs[i].key], [("ps", sb)])
                            else:
                                self.MM(ps[:, cc * 128:(cc + 1) * 128], Ks[i].ap[:, kt * 128:(kt + 1) * 128],
                                        Qs[i].ap[:, j * 128:(j + 1) * 128], True, kind != 0, [Ks[i].key, Qs[i].key], [("ps", sb)])
                                if kind == 0:
                                    b2 = 64 * (h % 2)
                                    self.MM(ps[:, cc * 128:(cc + 1) * 128], K2s[i].ap[b2:b2 + 64, kt * 128:(kt + 1) * 128],
                                            Q2s[i].ap[b2:b2 + 64, j * 128:(j + 1) * 128], False, True, [K2s[i].key, Q2s[i].key], [("ps", sb)])
                        if kind == 3:
                            for cc in range(4):
                                kt = 4 * grp + cc
                                self.ACT(pt.ap[:, cc * 128:(cc + 1) * 128], ps[:, cc * 128:(cc + 1) * 128], AF.Exp, [("ps", sb), bb.key], [pt.key],
                                         scale=scale, bias=bb.ap[:, kt:kt + 1])
                        else:
                            self.ACT(pt.ap, ps[:, 0:512], AF.Exp, [("ps", sb)], [pt.key], scale=scale)
                        pt3 = pt.ap.rearrange("p (a b) -> p a b", a=4, b=128)
                        if kind == 1:
                            self.TT("dve", pt3, pt3, selT.ap[:, sel_off[j] + 4 * grp:sel_off[j] + 4 * grp + 4, :], ALU.mult,
                                    [pt.key, selT.key], [pt.key])
                        if grp >= 4 * m:
                            t0 = 4 * (grp - 4 * m)
                            self.TT("pool" if kind == 1 else "dve", pt3, pt3, MASK.ap[:, par, t0:t0 + 4, :], ALU.mult, [pt.key, MASK.key], [pt.key])
                        for cc in range(4):
                            kt = 4 * grp + cc
                            self.MM(po[:, 0:129], pt.ap[:, cc * 128:(cc + 1) * 128], Vs[i].ap[:, kt, :], kt == 0, kt == nkt - 1,
                                    [pt.key, Vs[i].key], [("ps", pob)])
                ob = osb[(h * NT + j) % 3]
                rs = rsm[(h * NT + j) % 3]
                po = c.ps(pobanks[0])
                self.S.op("dve", lambda e_, o=rs.ap, pi=po[:, 128:129]: e_.reciprocal(out=o, in_=pi), [("ps", pobanks[0])], [rs.key])
                self.TS("dve", ob.ap, po[:, 0:128], rs.ap, None, ALU.mult, None, [("ps", pobanks[0]), rs.key], [ob.key])
                if kind == 2:
                    po1 = c.ps(pobanks[1])
                    rs1 = rsm[(h * NT + j + 1) % 3]
                    ob1 = osb[(h * NT + j + 1) % 3]
                    self.S.op("dve", lambda e_, o=rs1.ap, pi=po1[:, 128:129]: e_.reciprocal(out=o, in_=pi), [("ps", pobanks[1])], [rs1.key])
                    self.TS("dve", ob1.ap, po1[:, 0:128], rs1.ap, None, ALU.mult, None, [("ps", pobanks[1]), rs1.key], [ob1.key])
                    self.STT(ob.ap, ob1.ap, neglam.ap, ob.ap, ALU.mult, ALU.add, [ob1.key, neglam.key, ob.key], [ob.key])
                self.DMA(O[j * 128:(j + 1) * 128, h * 128:(h + 1) * 128], ob.ap, [ob.key], [("O", j)])
        c.release(mk)

    def s4_fox_prelude(self, tin):
        c = self.c
        NCK = c.alloc([128, 64, 16], F32, "NCK")
        cref = c.alloc([128, NT, 16], F32, "cref")
        mk = c.mark()
        lg = c.alloc([16, SEQ], F32, "lg")
        cum = c.alloc([16, SEQ], F32, "cum")
        ones = c.alloc([16, 1024], F32, "ones")
        self.MEMSET("dve", ones.ap, 1.0, [ones.key])
        L = tin["LOGF_all"]
        for m in range(4):
            self.DMA(lg.ap[:, 16 * m * 128:(16 * m + 8) * 128].rearrange("h (r p) -> h r p", r=8, p=128),
                     L[:, :, 2 * m * 128:(2 * m + 1) * 128].rearrange("r h p -> h r p"), [], [lg.key])
            for r in range(8):
                g = 16 * m + 15 - r
                self.DMA(lg.ap[:, g * 128:(g + 1) * 128], L[r, :, (2 * m + 1) * 128:(2 * m + 2) * 128], [], [lg.key])
        for s in range(8):
            init = 0.0 if s == 0 else cum.ap[:, s * 1024 - 1:s * 1024]
            self.S.op("dve", lambda e, s=s, init=init: e.tensor_tensor_scan(out=cum.ap[:, s * 1024:(s + 1) * 1024], data0=ones.ap,
                                                                              data1=lg.ap[:, s * 1024:(s + 1) * 1024], initial=init,
                                                                              op0=ALU.mult, op1=ALU.add),
                      [lg.key, ones.key, cum.key], [cum.key])
        id16 = self.ident[0:16, 0:16]
        for half in range(2):
            b = half
            ps = c.ps(b)
            for gg in range(32):
                g = 32 * half + gg
                self.TR(ps[:, gg * 16:(gg + 1) * 16], cum.ap[:, g * 128:(g + 1) * 128], id16, [cum.key, self.cst.key], [("ps", b)])
            for gg in range(32):
                g = 32 * half + gg
                m, t = divmod(g, 16)
                kt = 16 * m + (2 * t if t < 8 else 2 * (15 - t) + 1)
                self.TS("dve", NCK.ap[:, kt, :], ps[:, gg * 16:(gg + 1) * 16], -1.0, None, ALU.mult, None, [("ps", b)], [NCK.key])
        c63 = c.alloc([16, 64], F32, "c63")
        self.COPY("dve", c63.ap, cum.ap.rearrange("h (g p) -> h g p", g=64, p=128)[:, :, 63], [cum.key], [c63.key])
        ps = c.ps(2)
        self.TR(ps[0:64, 0:16], c63.ap, id16, [c63.key, self.cst.key], [("ps", 2)])
        c63T = c.alloc([64, 16], BF16, "c63T")
        self.COPY("dve", c63T.ap, ps[0:64, 0:16], [("ps", 2)], [c63T.key])
        SELs = c.alloc([64, NT, 128], BF16, "SELs")
        self.DMA(SELs.ap, tin["sel"].rearrange("j g k -> g j k"), [], [SELs.key])
        ps3 = c.ps(3)
        for j in range(NT):
            self.MM(ps3[:, j * 16:(j + 1) * 16], SELs.ap[:, j, :], c63T.ap, True, True, [SELs.key, c63T.key], [("ps", 3)])
        self.COPY("dve", cref.ap, ps3[:, 0:NT * 16].rearrange("p (j h) -> p j h", j=NT, h=16), [("ps", 3)], [cref.key])
        c.release(mk)
        return NCK, cref

    def s5_outproj(self, kind, O, XS, MOD, w_out, w, tin, layer_idx):
        c = self.c
        mk = c.mark()
        OT = c.alloc([128, 16, TPC], BF16, "OT")
        mk2 = c.mark()
        ob = [c.alloc([128, D], F32, "ob") for _ in range(2)]
        obb = [c.alloc([128, D], BF16, "obb") for _ in range(2)]
        gb = None
        if kind == 3:
            gb = [c.alloc([128, D], F32, "gb") for _ in range(2)]
        if kind == 2:
            lam_init = 0.8 - 0.6 * math.exp(-0.3 * layer_idx)
            sg = self.load_gain(w["diff_subln_g"], 128, "subg")
            self.TS("dve", sg.ap, sg.ap, 1.0 - lam_init, None, ALU.mult, None, [sg.key], [sg.key])
            scr = c.alloc([128, D], F32, "scr")
            rs = c.alloc([128, 16], F32, "rs")
        for t in range(NT):
            o = ob[t % 2]
            self.DMA(o.ap, O[t * 128:(t + 1) * 128, :], [("O", t)], [o.key])
            if kind == 3:
                g = gb[t % 2]
                self.DMA(g.ap, tin["GATE"][t * 128:(t + 1) * 128, :], [], [g.key])
                self.TT("dve", obb[t % 2].ap, o.ap, g.ap, ALU.mult, [o.key, g.key], [obb[t % 2].key])
            if kind == 2:
                o3 = o.ap.rearrange("p (h n) -> p h n", h=16, n=128)
                self.rstd_groups(o3, [o.key], 16, 128, scr, rs)
                self.TT("dve", o3, o3, bc_last(rs.ap, 128), ALU.mult, [o.key, rs.key], [o.key])
                self.TT("pool", obb[t % 2].ap.rearrange("p (h n) -> p h n", h=16, n=128), o3, bc_mid(sg.ap, 16), ALU.mult,
                        [o.key, sg.key], [obb[t % 2].key])
            if kind in (0, 1):
                self.COPY("act", obb[t % 2].ap, o.ap, [o.key], [obb[t % 2].key])
            self.transpose_f32(obb[t % 2].ap, [obb[t % 2].key], 16, OT, 0, t * 128, [0, 1])
        c.release(mk2)
        G1 = c.alloc([128, D], F32, "G1")
        self.DMA(G1.ap, MOD[:, 2 * D:3 * D], [("MOD", k) for k in range(24)], [G1.key])
        wb = [c.alloc([128, 16, 512], BF16, "wout") for _ in range(2)]
        xs = [c.alloc([128, 512], F32, "xs") for _ in range(3)]
        tm = [c.alloc([128, 512], F32, "tm") for _ in range(3)]
        cnt = [0]

        def epi(t, n0, nw, ps, bank):
            i = cnt[0] % 3
            cnt[0] += 1
            self.DMA(xs[i].ap, XS[t * 128:(t + 1) * 128, n0:n0 + nw], [("XS", t, n0)], [xs[i].key])
            self.TT("dve", tm[i].ap, ps, G1.ap[:, n0:n0 + nw], ALU.mult, [("ps", bank), G1.key], [tm[i].key])
            self.TT("pool", tm[i].ap, tm[i].ap, xs[i].ap, ALU.add, [tm[i].key, xs[i].key], [tm[i].key])
            self.DMA(XS[t * 128:(t + 1) * 128, n0:n0 + nw], tm[i].ap, [tm[i].key], [("XS", t, n0), ("XS", t)])

        self.linear_tm(OT, 0, 16, NT, w_out, D, epi, [2, 3, 4, 5], wb)
        c.release(mk)

    def s7_ffn(self, hT, Wgu, Wd, AT, XS, MOD, XOUT):
        c = self.c
        mk = c.mark()
        Wv = Wgu.rearrange("(kc p) n -> p kc n", p=128)
        wg = [c.alloc([128, 16, 256], BF16, "wg") for _ in range(2)]
        wu = [c.alloc([128, 16, 256], BF16, "wu") for _ in range(2)]
        sgb = [c.alloc([128, 512], F32, "sgb") for _ in range(2)]
        ab = [c.alloc([128, 512], BF16, "ab") for _ in range(2)]
        cnt = 0
        for fp in range(FFN // 256):
            i = fp % 2
            self.DMA(wg[i].ap, Wv[:, :, fp * 256:(fp + 1) * 256], [], [wg[i].key], eng="pool")
            self.DMA(wu[i].ap, Wv[:, :, FFN + fp * 256:FFN + (fp + 1) * 256], [], [wu[i].key], eng="pool")
            for fi in range(2):
                f = 2 * fp + fi
                for th in range(2):
                    bg = 2 * (cnt % 2)
                    bu = bg + 1
                    k = cnt % 2
                    cnt += 1
                    psg, psu = c.ps(bg), c.ps(bu)
                    for kc in range(16):
                        self.MM(psg[:, 0:512], wg[i].ap[:, kc, fi * 128:(fi + 1) * 128], hT.ap[:, kc, th * 512:(th + 1) * 512], kc == 0, kc == 15,
                                [wg[i].key, hT.key], [("ps", bg)])
                    for kc in range(16):
                        self.MM(psu[:, 0:512], wu[i].ap[:, kc, fi * 128:(fi + 1) * 128], hT.ap[:, kc, th * 512:(th + 1) * 512], kc == 0, kc == 15,
                                [wu[i].key, hT.key], [("ps", bu)])
                    self.ACT(sgb[k].ap, psg[:, 0:512], AF.Silu, [("ps", bg)], [sgb[k].key])
                    self.TT("dve", ab[k].ap, sgb[k].ap, psu[:, 0:512], ALU.mult, [sgb[k].key, ("ps", bu)], [ab[k].key])
                    self.DMA(AT[f * 128:(f + 1) * 128, th * 512:(th + 1) * 512], ab[k].ap, [ab[k].key], [("AT", th)])
        c.release(mk)
        G2 = c.alloc([128, D], F32, "G2")
        self.DMA(G2.ap, MOD[:, 5 * D:6 * D], [("MOD", k) for k in range(24)], [G2.key])
        ATs = c.alloc([128, 44, 512], BF16, "ATs")
        wd = [c.alloc([128, 11, 512], BF16, "wd") for _ in range(2)]
        xs = [c.alloc([128, 512], F32, "xs") for _ in range(3)]
        tm = [c.alloc([128, 512], F32, "tm") for _ in range(3)]
        Wdv = Wd.rearrange("(fc p) n -> p fc n", p=128)
        ATv = AT.rearrange("(fc p) t -> p fc t", p=128)
        wcnt = 0
        ecnt = 0
        for th in range(2):
            for q4 in range(4):
                self.DMA(ATs.ap[:, q4 * 11:(q4 + 1) * 11, :], ATv[:, q4 * 11:(q4 + 1) * 11, th * 512:(th + 1) * 512], [("AT", th)], [ATs.key])
            for nc4 in range(4):
                banks = [4 * (nc4 % 2) + tt for tt in range(4)]
                for fg in range(4):
                    wbuf = wd[wcnt % 2]
                    wcnt += 1
                    self.DMA(wbuf.ap, Wdv[:, fg * 11:(fg + 1) * 11, nc4 * 512:(nc4 + 1) * 512], [], [wbuf.key], eng="pool")
                    for tt in range(4):
                        ps = c.ps(banks[tt])
                        for fi in range(11):
                            fc = fg * 11 + fi
                            self.MM(ps[:, 0:512], ATs.ap[:, fc, tt * 128:(tt + 1) * 128], wbuf.ap[:, fi, :], fc == 0, fc == 43,
                                    [ATs.key, wbuf.key], [("ps", banks[tt])])
                for tt in range(4):
                    t = th * 4 + tt
                    n0 = nc4 * 512
                    i = ecnt % 3
                    ecnt += 1
                    ps = c.ps(banks[tt])
                    self.DMA(xs[i].ap, XS[t * 128:(t + 1) * 128, n0:n0 + 512], [("XS", t, n0), ("XS", t)], [xs[i].key])
                    self.TT("dve", tm[i].ap, ps[:, 0:512], G2.ap[:, n0:n0 + 512], ALU.mult, [("ps", banks[tt]), G2.key], [tm[i].key])
                    self.TT("pool", tm[i].ap, tm[i].ap, xs[i].ap, ALU.add, [tm[i].key, xs[i].key], [tm[i].key])
                    self.DMA(XS[t * 128:(t + 1) * 128, n0:n0 + 512], tm[i].ap, [tm[i].key], [("XS", t, n0), ("XS", t)])
                    if XOUT is not None:
                        self.DMA(XOUT[t * 128:(t + 1) * 128, n0:n0 + 512], tm[i].ap, [tm[i].key], [("XOUT", t, n0)])
        c.release(mk)


A_OUT = {
    0: [("QT", [16, 128, TPC], BF16), ("QT2", [8, 128, TPC], BF16), ("KT", [16, 128, TPC], BF16), ("KT2", [8, 128, TPC], BF16),
        ("V", [TPC, D], BF16)],
    1: [("QT", [16, 128, TPC], BF16), ("KT", [16, 128, TPC], BF16), ("V", [TPC, D], BF16), ("IQT", [8, 128, TPC], BF16),
        ("IKT", [1, 128, TPC], BF16), ("IW", [TPC, 16], F32)],
    2: [("QT", [16, 128, TPC], BF16), ("KT", [16, 128, TPC], BF16), ("V", [TPC, D], BF16)],
    3: [("QT", [16, 128, TPC], BF16), ("KT", [16, 128, TPC], BF16), ("V", [TPC, D], BF16), ("LOGF", [16, TPC], F32),
        ("GATE", [TPC, D], F32)],
}
GATHERED = {"KT", "KT2", "V", "IKT", "LOGF"}
A_WEIGHTS = {
    0: [("mla_w_in", [D, 1088]), ("mla_q_a_g", [512]), ("mla_kv_a_g", [512]), ("mla_w_q_b", [512, 3072]), ("mla_w_kv_b", [512, 4096]),
        ("mla_q_g", [192]), ("mla_k_g", [192])],
    1: [("dsa_w_in", [D, 7248]), ("dsa_q_g", [128]), ("dsa_k_g", [128]), ("dsa_idx_k_g", [64])],
    2: [("diff_w_in", [D, 6144]), ("diff_q_g", [64]), ("diff_k_g", [64])],
    3: [("fox_w_in", [D, 8208]), ("fox_b_f", [16]), ("fox_q_g", [128]), ("fox_k_g", [128])],
}
B_WEIGHTS = {
    0: [("mla_w_out", [D, D])],
    1: [("dsa_w_out", [D, D])],
    2: [("diff_w_out", [D, D]), ("diff_lambda_q1", [64]), ("diff_lambda_k1", [64]), ("diff_lambda_q2", [64]), ("diff_lambda_k2", [64]),
        ("diff_subln_g", [128])],
    3: [("fox_w_out", [D, D])],
}
W_IN_NAME = {0: "mla_w_in", 1: "dsa_w_in", 2: "diff_w_in", 3: "fox_w_in"}
W_OUT_NAME = {0: "mla_w_out", 1: "dsa_w_out", 2: "diff_w_out", 3: "fox_w_out"}
ROPES = {0: {"mla": (32, 384)}, 1: {"head": (16, 416), "idx": (8, 432)}, 2: {"diff": (8, 440)}, 3: {}}


def build_program(lb, la):
    nc = bass.Bass("TRN2", target_bir_lowering=False)
    with contextlib.ExitStack() as st:
        P = Prog(nc, st)
        c = P.c
        P.setup_consts()
        x_in = P.ext_in("x_in", [TPC, D], F32)
        c_ap = P.ext_in("c", [1, D], F32)
        XS = P.scratch("XS", [TPC, D], F32)
        MOD = P.scratch("MOD", [128, 6 * D], F32)
        mk = c.mark()
        xb = [c.alloc([128, D], F32, "xcp") for _ in range(2)]
        for t in range(NT):
            P.DMA(xb[t % 2].ap, x_in[t * 128:(t + 1) * 128, :], [], [xb[t % 2].key])
            P.DMA(XS[t * 128:(t + 1) * 128, :], xb[t % 2].ap, [xb[t % 2].key], [("XS", t)] + [("XS", t, n0) for n0 in range(0, D, 512)])
        c.release(mk)
        if lb is not None:
            kind = lb % 4
            tinb = {}
            for nm, shp, dt in A_OUT[kind]:
                if nm in GATHERED:
                    tinb[nm + "_all"] = P.ext_in("b_" + nm + "_all", [NCORES] + shp, dt)
                else:
                    tinb[nm] = P.ext_in("b_" + nm, shp, dt)
            tinb["maskT"] = P.ext_in("maskT", [2, 16, 128, 128], BF16)
            if kind == 1:
                tinb["pen"] = P.ext_in("pen", [2, 128, 2048], BF16)
            if kind == 3:
                tinb["sel"] = P.ext_in("sel", [NT, 64, 128], BF16)
            wB = {}
            for nm, shp in B_WEIGHTS[kind]:
                wB[nm] = P.ext_in("b_" + nm, shp, F32)
            adaw = P.ext_in("b_ada_w", [D, 6 * D], F32)
            adab = P.ext_in("b_ada_b", [6 * D], F32)
            lnf = P.ext_in("b_ln_ffn_g", [D], F32)
            Wgu = P.ext_in("b_ffn_w_gate_up", [D, 2 * FFN], F32)
            Wd = P.ext_in("b_ffn_w_down", [FFN, D], F32)
            x_out = P.ext_out("x_out", [TPC, D], F32)
            O = P.scratch("O", [TPC, D], F32)
            AT = P.scratch("AT", [FFN, TPC], BF16)
            P.stage_mod(c_ap, adaw, adab, MOD, 2 * D, 6 * D)
            P.s4_attention(kind, tinb, O, wB, lb)
            P.s5_outproj(kind, O, XS, MOD, wB[W_OUT_NAME[kind]], wB, tinb, lb)
            mk = c.mark()
            hT = c.alloc([128, 16, TPC], BF16, "h2T")
            mk2 = P.stage_norm(XS, lnf, MOD, 3 * D, 4 * D, hT)
            c.release(mk2)
            P.s7_ffn(hT, Wgu, Wd, AT, XS, MOD, x_out)
            c.release(mk)
        if la is not None:
            kind = la % 4
            P.setup_rope(ROPES[kind])
            wA = {}
            for nm, shp in A_WEIGHTS[kind]:
                wA[nm] = P.ext_in("a_" + nm, shp, F32)
            adaw = P.ext_in("a_ada_w", [D, 6 * D], F32)
            adab = P.ext_in("a_ada_b", [6 * D], F32)
            lnm = P.ext_in("a_ln_mix_g", [D], F32)
            outs = {}
            for nm, shp, dt in A_OUT[kind]:
                outs[nm] = P.ext_out("a_" + nm, shp, dt)
            NP = MIX_NPROJ[kind]
            PROJ = P.scratch("PROJ", [TPC, NP], F32)
            P.stage_mod(c_ap, adaw, adab, MOD, 0, 2 * D)
            mk = c.mark()
            hT = c.alloc([128, 16, TPC], BF16, "hT")
            mk2 = P.stage_norm(XS, lnm, MOD, 0, D, hT)
            c.release(mk2)
            P.stage_proj(hT, 16, wA[W_IN_NAME[kind]], NP, PROJ, "PROJ")
            c.release(mk)
            if kind == 0:
                Q2 = P.scratch("Q2", [TPC, 3072], F32)
                KV2 = P.scratch("KV2", [TPC, 4096], F32)
                P.s3_mla(PROJ, wA, outs, Q2, KV2)
            elif kind == 1:
                P.s3_dsa(PROJ, wA, outs)
            elif kind == 2:
                P.s3_diff(PROJ, wA, outs)
            else:
                P.s3_fox(PROJ, wA, outs)
        P.S.emit()
        meta = {"in": {k: (v[1], v[2]) for k, v in P.tin.items()}, "out": {k: (v[1], v[2]) for k, v in P.tout.items()}}
    return nc, meta


def _consts():
    cst = np.zeros((128, 512), np.float32)
    cst[:, 0:128] = np.eye(128, dtype=np.float32)
    theta = np.float32(500000.0)
    for off, rot in ((384, 64), (416, 32), (432, 16), (440, 16)):
        inv = theta ** (-(np.arange(0, rot, 2, dtype=np.float32)) / np.float32(rot))
        cst[:, off:off + rot // 2] = inv.astype(np.float32)[None, :]
    return cst


def _masks(r):
    bf = ml_dtypes.bfloat16
    maskT = np.zeros((2, 16, 128, 128), np.float32)
    pen = np.zeros((2, 128, 2048), np.float32)
    tri = (np.arange(128)[None, :] >= np.arange(128)[:, None]).astype(np.float32)
    for par in range(2):
        gq = r if par == 0 else 15 - r
        for kt in range(16):
            rr, kp = divmod(kt, 2)
            gk = rr if kp == 0 else 15 - rr
            if gk < gq:
                maskT[par, kt] = 1.0
            elif gk == gq:
                maskT[par, kt] = tri
            pen[par, :, kt * 128:(kt + 1) * 128] = (1.0 - maskT[par, kt].T) * NEG
    sel = np.zeros((NT, 64, 128), np.float32)
    for j, g in enumerate(own_tiles(r)):
        sel[j, g, :] = 1.0
    return maskT.astype(bf), pen.astype(bf), sel.astype(bf)


_PROG_CACHE = {}


def _get_prog(lb, la):
    key = (lb, la)
    if key not in _PROG_CACHE:
        _PROG_CACHE[key] = build_program(lb, la)
    return _PROG_CACHE[key]


def _layer_weights(inputs, i, names, prefix):
    out = {}
    for nm, _ in names:
        out[prefix + nm] = np.ascontiguousarray(inputs[nm][i // 4])
    return out


def kernel(**inputs):
    x = np.asarray(inputs["x"])[0]
    pos = np.asarray(inputs["positions"])[0].astype(np.int32)
    cvec = np.ascontiguousarray(np.asarray(inputs["c"]), dtype=np.float32)
    tiles = [own_tiles(r) for r in range(NCORES)]
    tok_idx = [np.concatenate([np.arange(g * 128, (g + 1) * 128) for g in tiles[r]]) for r in range(NCORES)]
    xs = [np.ascontiguousarray(x[tok_idx[r]]) for r in range(NCORES)]
    ps = [np.ascontiguousarray(pos[tok_idx[r]]) for r in range(NCORES)]
    cst = _consts()
    masks = [_masks(r) for r in range(NCORES)]
    carry = [dict() for _ in range(NCORES)]
    for step in range(5):
        lb = step - 1 if step >= 1 else None
        la = step if step <= 3 else None
        nc, meta = _get_prog(lb, la)
        in_maps = []
        for r in range(NCORES):
            m = {"consts": cst, "x_in": xs[r], "c": cvec}
            if lb is not None:
                kind = lb % 4
                for nm, shp, dt in A_OUT[kind]:
                    if nm in GATHERED:
                        m["b_" + nm + "_all"] = gathered[nm]
                    else:
                        m["b_" + nm] = carry[r][nm]
                m["maskT"] = masks[r][0]
                if kind == 1:
                    m["pen"] = masks[r][1]
                if kind == 3:
                    m["sel"] = masks[r][2]
                m.update(_layer_weights(inputs, lb, B_WEIGHTS[kind], "b_"))
                m["b_ada_w"] = np.ascontiguousarray(inputs["ada_w"][lb])
                m["b_ada_b"] = np.ascontiguousarray(inputs["ada_b"][lb])
                m["b_ln_ffn_g"] = np.ascontiguousarray(inputs["ln_ffn_g"][lb])
                m["b_ffn_w_gate_up"] = np.ascontiguousarray(inputs["ffn_w_gate_up"][lb])
                m["b_ffn_w_down"] = np.ascontiguousarray(inputs["ffn_w_down"][lb])
            if la is not None:
                kind = la % 4
                m["positions"] = ps[r]
                m.update(_layer_weights(inputs, la, A_WEIGHTS[kind], "a_"))
                m["a_ada_w"] = np.ascontiguousarray(inputs["ada_w"][la])
                m["a_ada_b"] = np.ascontiguousarray(inputs["ada_b"][la])
                m["a_ln_mix_g"] = np.ascontiguousarray(inputs["ln_mix_g"][la])
            in_maps.append(m)
        res = run_bass_kernel_spmd(nc, in_maps, core_ids=list(range(NCORES)))
        outs = res.results
        if lb is not None:
            xs = [np.asarray(outs[r]["x_out"]) for r in range(NCORES)]
        if la is not None:
            kind = la % 4
            carry = [{nm: np.asarray(outs[r]["a_" + nm]) for nm, _, _ in A_OUT[kind]} for r in range(NCORES)]
            gathered = {nm: np.ascontiguousarray(np.stack([carry[r][nm] for r in range(NCORES)])) for nm, _, _ in A_OUT[kind] if nm in GATHERED}
    out = np.zeros((SEQ, D), np.float32)
    for r in range(NCORES):
        out[tok_idx[r]] = xs[r]
    return out[None]
```
